# Optimizing a Trainium2 kernel written in Bass

```python
import math
import jax, jax.numpy as jnp
from jax import lax
import numpy as np

D_MODEL = 1024
BATCH = 8
SEQ = 2048
DEPTH = 2
DEC_BATCH = 32
DEC_SEQ = 1
PAST_LEN = 8192
PAGE_SIZE = 128

N_AB = (DEPTH + 1) // 2
N_C = DEPTH // 2
SB_HEADS = 8
SB_HEAD_DIM = D_MODEL // 16
SB_WIDTH = SB_HEADS * SB_HEAD_DIM
SB_BIAS_INIT = -7.0
POOL_WIDTH = D_MODEL // 2
POOL_WINDOWS = (2, 4, 8, 16)
POOL_GROUPS = len(POOL_WINDOWS)
POOL_GROUP_DIM = POOL_WIDTH // POOL_GROUPS
POOL_STATE = max(POOL_WINDOWS) - 1
CONV_WIDTH = D_MODEL
CONV_KERNEL = 31
CONV_STATE = CONV_KERNEL - 1
Q_BLOCK = 128
LN_EPS = 1e-5
ALPHA = (2.0 * DEPTH) ** 0.25
BETA_INIT = (8.0 * DEPTH) ** -0.25
AB_IN = 4 * SB_WIDTH + 2 * POOL_WIDTH
C_IN = 3 * CONV_WIDTH

kernel_name = "stickbreak_pool_conformer_hybrid_step"


def layer_norm(x, g, b):
    xf = x.astype(jnp.float32)
    mu = jnp.mean(xf, axis=-1, keepdims=True)
    var = jnp.mean(jnp.square(xf - mu), axis=-1, keepdims=True)
    y = (xf - mu) * lax.rsqrt(var + LN_EPS)
    return (y * g.astype(jnp.float32) + b.astype(jnp.float32)).astype(x.dtype)


def stick_breaking(q, k, v, q_pos, k_pos, sb_bias):
    z = jnp.einsum('bqhd,bkhd->bhqk', q, k).astype(jnp.float32) / math.sqrt(SB_HEAD_DIM)
    z = z + sb_bias.astype(jnp.float32)[None, :, None, None]
    causal = k_pos[None, :] < q_pos[:, None]
    log_keep = jnp.where(causal, jax.nn.log_sigmoid(-z), 0.0)
    after = lax.cumsum(log_keep, axis=3, reverse=True) - log_keep
    w = jnp.where(causal, jnp.exp(jax.nn.log_sigmoid(z) + after), 0.0)
    return jnp.einsum('bhqk,bkhd->bqhd', w.astype(v.dtype), v)


def stick_breaking_prompt(q, k, v, sb_bias):
    B, S, H, Dh = q.shape
    nb = S // Q_BLOCK
    qb = q.reshape(B, nb, Q_BLOCK, H, Dh).transpose(1, 0, 2, 3, 4)
    k_pos = jnp.arange(S)

    def block(args):
        qi, bi = args
        q_pos = bi * Q_BLOCK + jnp.arange(Q_BLOCK)
        return stick_breaking(qi, k, v, q_pos, k_pos, sb_bias)

    out = lax.map(block, (qb, jnp.arange(nb)))
    return out.transpose(1, 0, 2, 3, 4).reshape(B, S, H, Dh)


def multiscale_pool(u_ext, start_pos, w_pool, pool_scale):
    B, L, _ = u_ext.shape
    T = L - POOL_STATE
    uf = u_ext.astype(jnp.float32)
    cs = jnp.concatenate([jnp.zeros((B, 1, POOL_WIDTH), jnp.float32), jnp.cumsum(uf, axis=1)], axis=1)
    pos = start_pos + jnp.arange(T)
    diffs = []
    for g, w in enumerate(POOL_WINDOWS):
        c0, c1 = g * POOL_GROUP_DIM, (g + 1) * POOL_GROUP_DIM
        hi = cs[:, POOL_STATE + 1:POOL_STATE + 1 + T, c0:c1]
        lo = cs[:, POOL_STATE + 1 - w:POOL_STATE + 1 - w + T, c0:c1]
        count = jnp.minimum(w, pos + 1).astype(jnp.float32)
        diffs.append((hi - lo) / count[None, :, None] - uf[:, POOL_STATE:, c0:c1])
    d = jnp.stack(diffs, axis=2)
    y = jnp.einsum('btgc,gcd->btgd', d.astype(w_pool.dtype), w_pool) * pool_scale
    return y.reshape(B, T, POOL_WIDTH).astype(u_ext.dtype)


def ab_mix(x, pool_prev, kv_past, start_pos, w_in, sb_bias, w_pool, pool_scale, w_out):
    B, T, _ = x.shape
    p = x @ w_in
    q, k, v, g_a, u_b, g_b = jnp.split(p, [SB_WIDTH, 2 * SB_WIDTH, 3 * SB_WIDTH, 4 * SB_WIDTH, 4 * SB_WIDTH + POOL_WIDTH], axis=-1)
    q = q.reshape(B, T, SB_HEADS, SB_HEAD_DIM)
    k = k.reshape(B, T, SB_HEADS, SB_HEAD_DIM)
    v = v.reshape(B, T, SB_HEADS, SB_HEAD_DIM)
    if kv_past is None:
        o_a = stick_breaking_prompt(q, k, v, sb_bias)
    else:
        k_past, v_past = kv_past
        n_past = k_past.shape[1]
        kk = jnp.concatenate([k_past.astype(k.dtype), k], axis=1)
        vv = jnp.concatenate([v_past.astype(v.dtype), v], axis=1)
        o_a = stick_breaking(q, kk, vv, n_past + jnp.arange(T), jnp.arange(n_past + T), sb_bias)
    u_ext = jnp.concatenate([pool_prev.astype(u_b.dtype), u_b], axis=1)
    o_b = multiscale_pool(u_ext, start_pos, w_pool, pool_scale)
    h = jnp.concatenate([o_a.reshape(B, T, SB_WIDTH) * jax.nn.silu(g_a), o_b * jax.nn.silu(g_b)], axis=-1)
    return h @ w_out, k, v, u_ext[:, -POOL_STATE:]


def c_mix(x, conv_prev, w_in, w_dw, b_dw, cn_g, cn_b, w_out):
    p = x @ w_in
    a, a_gate, gate = jnp.split(p, [CONV_WIDTH, 2 * CONV_WIDTH], axis=-1)
    h = a * jax.nn.sigmoid(a_gate)
    h_ext = jnp.concatenate([conv_prev.astype(h.dtype), h], axis=1)
    c = lax.conv_general_dilated(h_ext, w_dw[:, None, :].astype(h.dtype), window_strides=(1,), padding='VALID',
                                 dimension_numbers=('NWC', 'WIO', 'NWC'), feature_group_count=CONV_WIDTH)
    c = jax.nn.silu(layer_norm(c + b_dw, cn_g, cn_b))
    return (c * jax.nn.silu(gate)) @ w_out, h_ext[:, -CONV_STATE:]


def setup_inputs(seed: int = 0) -> dict:
    key = jax.random.key(seed)
    ks = jax.random.split(key, 24)
    n_pages = PAST_LEN // PAGE_SIZE
    n_pool = (DEC_BATCH * n_pages * 5) // 4
    nrm = lambda k, s: jax.random.normal(k, s, jnp.float32)
    page_table = jax.random.permutation(ks[0], n_pool)[:DEC_BATCH * n_pages].reshape(DEC_BATCH, n_pages).astype(jnp.int32)
    return {
        "x_prompt": nrm(ks[1], (BATCH, SEQ, D_MODEL)),
        "x_sample": nrm(ks[2], (DEC_BATCH, DEC_SEQ, D_MODEL)),
        "cache_k": nrm(ks[3], (N_AB, n_pool, PAGE_SIZE, SB_HEADS, SB_HEAD_DIM)),
        "cache_v": nrm(ks[4], (N_AB, n_pool, PAGE_SIZE, SB_HEADS, SB_HEAD_DIM)),
        "state_pool": nrm(ks[5], (N_AB, DEC_BATCH, POOL_STATE, POOL_WIDTH)),
        "state_conv": 0.5 * nrm(ks[6], (N_C, DEC_BATCH, CONV_STATE, CONV_WIDTH)),
        "page_table": page_table,
        "w_in_ab": nrm(ks[7], (N_AB, D_MODEL, AB_IN)) * D_MODEL ** -0.5,
        "sb_bias": SB_BIAS_INIT + 0.1 * nrm(ks[21], (N_AB, SB_HEADS)),
        "w_pool": nrm(ks[8], (N_AB, POOL_GROUPS, POOL_GROUP_DIM, POOL_GROUP_DIM)) * POOL_GROUP_DIM ** -0.5,
        "pool_scale": 1.0 + 0.1 * nrm(ks[9], (N_AB, POOL_GROUPS, POOL_GROUP_DIM)),
        "w_out_ab": nrm(ks[10], (N_AB, SB_WIDTH + POOL_WIDTH, D_MODEL)) * (SB_WIDTH + POOL_WIDTH) ** -0.5 * BETA_INIT,
        "ln_ab_g": 1.0 + 0.05 * nrm(ks[11], (N_AB, D_MODEL)),
        "ln_ab_b": 0.02 * nrm(ks[12], (N_AB, D_MODEL)),
        "w_in_c": nrm(ks[13], (N_C, D_MODEL, C_IN)) * D_MODEL ** -0.5,
        "w_dw": nrm(ks[14], (N_C, CONV_KERNEL, CONV_WIDTH)) * CONV_KERNEL ** -0.5,
        "b_dw": 0.02 * nrm(ks[15], (N_C, CONV_WIDTH)),
        "conv_norm_g": 1.0 + 0.05 * nrm(ks[16], (N_C, CONV_WIDTH)),
        "conv_norm_b": 0.02 * nrm(ks[17], (N_C, CONV_WIDTH)),
        "w_out_c": nrm(ks[18], (N_C, CONV_WIDTH, D_MODEL)) * CONV_WIDTH ** -0.5 * BETA_INIT,
        "ln_c_g": 1.0 + 0.05 * nrm(ks[19], (N_C, D_MODEL)),
        "ln_c_b": 0.02 * nrm(ks[20], (N_C, D_MODEL)),
    }


def reference(x_prompt, x_sample, cache_k, cache_v, state_pool, state_conv, page_table,
              w_in_ab, sb_bias, w_pool, pool_scale, w_out_ab, ln_ab_g, ln_ab_b,
              w_in_c, w_dw, b_dw, conv_norm_g, conv_norm_b, w_out_c, ln_c_g, ln_c_b):
    xp, xs = x_prompt, x_sample
    bp, bs = xp.shape[0], xs.shape[0]
    kp_l, vp_l, ks_l, vs_l, pp_l, ps_l, cp_l, cs_l = [], [], [], [], [], [], [], []
    for layer in range(DEPTH):
        i = layer // 2
        if layer % 2 == 0:
            prm = (w_in_ab[i], sb_bias[i], w_pool[i], pool_scale[i], w_out_ab[i])
            zero_pool = jnp.zeros((bp, POOL_STATE, POOL_WIDTH), xp.dtype)
            op, kp, vp, pp = ab_mix(xp, zero_pool, None, 0, *prm)
            k_past = cache_k[i][page_table].reshape(bs, PAST_LEN, SB_HEADS, SB_HEAD_DIM)
            v_past = cache_v[i][page_table].reshape(bs, PAST_LEN, SB_HEADS, SB_HEAD_DIM)
            os_, ksn, vsn, psn = ab_mix(xs, state_pool[i], (k_past, v_past), PAST_LEN, *prm)
            xp = layer_norm(ALPHA * xp + op, ln_ab_g[i], ln_ab_b[i])
            xs = layer_norm(ALPHA * xs + os_, ln_ab_g[i], ln_ab_b[i])
            kp_l.append(kp); vp_l.append(vp); ks_l.append(ksn); vs_l.append(vsn)
            pp_l.append(pp); ps_l.append(psn)
        else:
            prm = (w_in_c[i], w_dw[i], b_dw[i], conv_norm_g[i], conv_norm_b[i], w_out_c[i])
            zero_conv = jnp.zeros((bp, CONV_STATE, CONV_WIDTH), xp.dtype)
            op, cp = c_mix(xp, zero_conv, *prm)
            os_, csn = c_mix(xs, state_conv[i], *prm)
            xp = layer_norm(ALPHA * xp + op, ln_c_g[i], ln_c_b[i])
            xs = layer_norm(ALPHA * xs + os_, ln_c_g[i], ln_c_b[i])
            cp_l.append(cp); cs_l.append(csn)
    return (xp, xs, jnp.stack(kp_l), jnp.stack(vp_l), jnp.stack(ks_l), jnp.stack(vs_l),
            jnp.stack(pp_l), jnp.stack(ps_l), jnp.stack(cp_l), jnp.stack(cs_l))
```

```python
import numpy as np
import concourse.bass as bass
import concourse.mybir as mybir
from concourse.bass_utils import run_bass_kernel_spmd

F32 = mybir.dt.float32
BF16 = mybir.dt.bfloat16
I32 = mybir.dt.int32
AF = mybir.ActivationFunctionType
ALU = mybir.AluOpType
AX = mybir.AxisListType

S = 2048
D = 1024
NB = 16
NTC = 4
NPAGE = 64
ALPHA = (2.0 * 2) ** 0.25
EPS = 1e-5
DEBUG = False
import os
STOP = os.environ.get('KSTOP', '')
SKIP = os.environ.get('KSKIP', '').split(',')
OQ = os.environ.get('KOQ', 'sp')
NCACHE = 256 * 128 if os.environ.get('KSMALL') else 2560 * 128


class Buf:
    def __init__(self, name, init=(), excl=False):
        self.name = name
        self.excl = excl
        self.w = None
        self.r = {}
        for t in init:
            if t is not None:
                self._addr(t)

    def _addr(self, t):
        key = id(t[0])
        if key not in self.r or self.r[key][1] < t[1]:
            self.r[key] = t

    def all_tokens(self):
        return [self.w] + list(self.r.values())


class Chan:
    def __init__(self, sem):
        self.sem = sem
        self.n = 0


class K:
    ENGS = ["pe", "act", "dve", "pool", "sp"]

    def __init__(self, nc):
        self.nc = nc
        self.ops = {e: [] for e in self.ENGS}
        self.cnt = {e: 0 for e in self.ENGS}
        self.sems = {}
        self.lastw = {e: {} for e in self.ENGS}
        self.ctx = []
        self.nsem = 0
        for e in self.ENGS:
            self.sems[e] = self.sem("s_" + e)

    def sem(self, name):
        cm = self.nc.semaphore(name)
        s = cm.__enter__()
        self.ctx.append(cm)
        self.nsem += 1
        return s

    def chan(self, name):
        return Chan(self.sem(name))

    def _waits(self, eng, deps):
        w = []
        for d in deps:
            if d is None:
                continue
            s, v = d
            if eng == "pe" and s is self.sems["pe"]:
                continue
            key = id(s)
            if self.lastw[eng].get(key, 0) < v:
                self.lastw[eng][key] = v
                w.append((s, v))
        return w

    def _deps(self, r, w, deps):
        d = list(deps)
        for b in r:
            if b.excl:
                d.extend(b.all_tokens())
            else:
                d.append(b.w)
        for b in w:
            d.extend(b.all_tokens())
        return d

    def _commit(self, tok, r, w):
        for b in r:
            b._addr(tok)
        for b in w:
            b.w = tok
            b.r = {}

    def op(self, eng, fn, r=(), w=(), deps=()):
        ws = self._waits(eng, self._deps(r, w, deps))
        self.cnt[eng] += 1
        tok = (self.sems[eng], self.cnt[eng])
        mysem = self.sems[eng]

        def run(e):
            for s, v in ws:
                e.wait_ge(s, v)
            ins = fn(e)
            ins.then_inc(mysem, 1)
        self.ops[eng].append(run)
        self._commit(tok, r, w)
        return tok

    def dma(self, eng, chan, fn, r=(), w=(), deps=()):
        ws = self._waits(eng, self._deps(r, w, deps))
        chan.n += 16
        tok = (chan.sem, chan.n)
        s0 = chan.sem

        def run(e):
            for s, v in ws:
                e.wait_ge(s, v)
            fn(e).then_inc(s0, 16)
        self.ops[eng].append(run)
        self._commit(tok, r, w)
        return tok

    def emit(self, final_deps):
        ws = self._waits("sp", final_deps)

        def fin(e):
            for s, v in ws:
                e.wait_ge(s, v)
        self.ops["sp"].append(fin)
        nc = self.nc
        with nc.Block() as block:
            @block.tensor
            def _(e):
                for f in self.ops["pe"]:
                    f(e)

            @block.scalar
            def _(e):
                for f in self.ops["act"]:
                    f(e)

            @block.vector
            def _(e):
                for f in self.ops["dve"]:
                    f(e)

            @block.gpsimd
            def _(e):
                for f in self.ops["pool"]:
                    f(e)

            @block.sync
            def _(e):
                for f in self.ops["sp"]:
                    f(e)
        for cm in reversed(self.ctx):
            cm.__exit__(None, None, None)


def build_nc():
    nc = bass.Bass("TRN2", target_bir_lowering=False)
    if os.environ.get("KPRECOOK", "1") == "0":
        nc.dge_precook = False
    k = K(nc)
    cms = []

    def din(name, shape, dt=F32):
        return nc.dram_tensor(name, shape, dt, kind="ExternalInput").ap()

    def dout(name, shape, dt=F32):
        return nc.dram_tensor(name, shape, dt, kind="ExternalOutput").ap()

    def sb(name, shape, dt):
        cm = nc.sbuf_tensor(name, shape, dt)
        t = cm.__enter__()
        cms.append(cm)
        return t

    x_p = din("x_p", [S, D])
    x_s = din("x_s", [4, D])
    cache_k = din("cache_k", [NCACHE, 512])
    cache_v = din("cache_v", [NCACHE, 512])
    st_pool = din("st_pool", [4, 15, 512])
    st_conv = din("st_conv", [4, 30, D])
    pt_in = din("pt", [128, 256], I32)
    w_in_ab = din("w_in_ab", [D, 3072])
    w_out_ab = din("w_out_ab", [D, D])
    w_in_c = din("w_in_c", [D, 3072])
    w_out_c = din("w_out_c", [D, D])
    wpool_in = din("wpool", [128, 4, 128])
    sbb_in = din("sbb", [128, 8])
    sbb512_in = din("sbb512", [128, 512])
    lnp_in = din("lnp", [4, 128, D])
    pscale_in = din("pscale", [128, 4])
    ps4_in = din("pscale4", [4, 512])
    cvec_in = din("cvec", [128, 3, 8])
    cvec4_in = din("cvec4", [3, 4, D])
    wdwT_in = din("wdwT", [128, 8, 31])
    wdwrep_in = din("wdwrep", [124, D])
    rct_in = din("rct", [128, 4, 16])
    pind_in = din("pind", [64, 16])

    y_p = dout("y_p", [S, D])
    y_s = dout("y_s", [4, D])
    k_p = dout("k_p", [S, 512])
    v_p = dout("v_p", [S, 512])
    k_s = dout("k_s", [4, 512])
    v_s = dout("v_s", [4, 512])
    pool_p = dout("pool_p", [15, 512])
    pool_s = dout("pool_s", [4, 15, 512])
    conv_p = dout("conv_p", [30, D])
    conv_s = dout("conv_s", [4, 30, D])
    if DEBUG:
        dbg_h = dout("dbg_h", [128, 8 * S], BF16)

    RW1 = sb("RW1", [128, 8192], BF16)
    RW2 = sb("RW2", [128, 16384], BF16)
    RQ = sb("RQ", [128, 32 * 1024], BF16)
    RH = sb("RH", [128, 16384], BF16)
    RC = sb("RC", [128, 16384], BF16)
    LNP = sb("LNP", [128, 4, D], F32)
    SM = sb("SM", [128, 1024 + 528], F32)
    CONST = sb("CONST", [128, 2048], BF16)

    def f32v(region, off, n):
        return region[:, off:off + 2 * n].bitcast(F32)

    ident = CONST[:, 0:128]
    trineg = CONST[:, 128:256]
    onesneg = CONST[:, 256:384]
    wpool_bf = CONST[:, 384:896].rearrange("p (g d) -> p g d", g=4)
    identf = f32v(CONST, 896, 128)
    trif = f32v(CONST, 1152, 128)
    onesf = f32v(CONST, 1408, 128)
    cmisc = sb("cmisc", [128, 512], F32)
    sbb = cmisc[:, 0:8]
    pscale = cmisc[:, 8:12]
    cvec = cmisc[:, 16:40].rearrange("p (a f) -> p a f", a=3)
    rct = cmisc[:, 64:128].rearrange("p (g c) -> p g c", g=4)
    wdwT = cmisc[:, 128:376].rearrange("p (f t) -> p f t", f=8)
    epsb = cmisc[:, 380:381]
    idx = sb("idx", [128, 256], I32)

    PS = []
    for i in range(8):
        cm = nc.psum_tensor("ps%d" % i, [128, 512], F32)
        PS.append(cm.__enter__())
        cms.append(cm)
    PSB = [k_ for k_ in range(8)]
    psb = [Buf("ps%d" % i, excl=True) for i in range(8)]

    out_ch = k.chan("out")
    b_const = Buf("const")
    ch_c = k.chan("c_in")

    def ld(eng, out, in_, chan, r=(), w=()):
        return k.dma(eng, chan, lambda e: e.dma_start(out=out, in_=in_), r=r, w=w)

    ld("sp", sbb, sbb_in, ch_c, w=[b_const])
    ld("sp", pscale, pscale_in, ch_c, w=[])
    ld("sp", cvec, cvec_in, ch_c)
    if 'c2' not in SKIP:
        ld("sp", rct, rct_in, ch_c)
        ld("sp", wdwT, wdwT_in, ch_c)
    ld("sp", LNP[:], lnp_in.rearrange("a p d -> p a d"), ch_c)
    ld("sp", idx[:], pt_in, ch_c)
    t_cin = (ch_c.sem, ch_c.n)
    b_const.w = t_cin
    ch_wp = k.chan("wp")
    k.dma("pool", ch_wp, lambda e: e.dma_start(out=wpool_bf, in_=wpool_in), w=[b_const])
    b_const.w = None
    t_wp = (ch_wp.sem, ch_wp.n)

    def mk_ident(t, val, pattern, cm, cmp_):
        k.op("pool", lambda e: e.memset(t, val), w=[b_const])
        k.op("pool", lambda e: e.affine_select(out=t, in_=t, pattern=pattern, compare_op=cmp_, fill=0.0,
                                               base=0, channel_multiplier=cm), w=[b_const])
    mk_ident(ident, 1.0, [[1, 128]], -1, ALU.is_equal)
    mk_ident(identf, 1.0, [[1, 128]], -1, ALU.is_equal)
    mk_ident(trineg, -1.0, [[-1, 128]], 1, ALU.is_ge)
    mk_ident(trif, 1.0, [[-1, 128]], 1, ALU.is_ge)
    k.op("pool", lambda e: e.memset(onesneg, -1.0), w=[b_const])
    k.op("pool", lambda e: e.memset(onesf, 1.0), w=[b_const])
    k.op("pool", lambda e: e.memset(epsb, EPS), w=[b_const])
    t_const = b_const.w
    CD = [t_cin, t_wp, t_const]

    XT = RW2[:, :].rearrange("p (c t) -> p c t", c=8)
    QT = RQ[:, 0:8192].rearrange("p (c t) -> p c t", c=4)
    KT = RQ[:, 8192:16384].rearrange("p (c t) -> p c t", c=4)
    V = RQ[:, 16384:24576].rearrange("p (b n) -> p b n", b=16)
    H = RH[:, :].rearrange("p (c t) -> p c t", c=8)
    b_XT = [Buf("XT%d" % i) for i in range(NB)]
    b_QT = Buf("QT")
    b_KT = Buf("KT")
    b_V = Buf("V")
    b_H = [[Buf("H%d_%d" % (c, t)) for t in range(NTC)] for c in range(8)]
    xstg = [f32v(RC, 0, 1024), f32v(RC, 2048, 1024)]
    xbt = [RC[:, 4096:5120], RC[:, 5120:6144]]
    Ubuf = f32v(RC, 6144, 528)
    Sa = f32v(RC, 7200, 528)
    Sb_ = f32v(RC, 8256, 528)
    Dd = RC[:, 9312:9824]
    kst = [f32v(RC, 9824, 512), f32v(RC, 10848, 512)]
    tmp16 = f32v(RC, 11872, 16)
    b_xstg = [Buf("xstg0"), Buf("xstg1")]
    b_xbt = [Buf("xbt0"), Buf("xbt1")]
    b_U = Buf("U")
    b_Sa = Buf("Sa")
    b_Sb = Buf("Sb")
    b_Dd = Buf("Dd")
    b_kst = [Buf("kst0"), Buf("kst1")]
    ch_x = [k.chan("x0"), k.chan("x1")]
    ch_kst = [k.chan("kst0"), k.chan("kst1")]
    Wsec = [RW1[:, 0:4096].rearrange("p (c n) -> p c n", c=8), RW1[:, 4096:8192].rearrange("p (c n) -> p c n", c=8)]
    b_Wsec = [Buf("Wsec0"), Buf("Wsec1")]
    ch_W = [k.chan("W0"), k.chan("W1")]

    xs_f = f32v(RC, 13952, 1024)[0:4, :]
    ps_s = [SM[0:4, 0:512]]
    kvs = [f32v(RC, 11904, 512)[0:4, :], f32v(RC, 12928, 512)[0:4, :]]
    xs_b = SM[0:4, 1024:1536].bitcast(BF16)
    XsT = SM[:, 1536:1552].bitcast(BF16).rearrange("p (c t) -> p c t", c=8)
    b_xs = Buf("xs")
    b_XsT = Buf("XsT")
    b_pss = [Buf("pss0")]
    b_kvs = [Buf("kvs0"), Buf("kvs1")]

    def wsec_load(w_dram, col0, slot, ncol=512, dcol=0):
        W_ = Wsec[slot]

        def fn(e):
            return e.dma_start(out=W_[:, :, dcol:dcol + ncol],
                               in_=w_dram[:, col0:col0 + ncol].rearrange("(c p) n -> p c n", p=128))
        return k.dma("pool", ch_W[slot], fn, w=[b_Wsec[slot]])

    ch_xs = k.chan("xs")
    ld("sp", xs_f, x_s, ch_xs, w=[b_xs])
    wsec_load(w_in_ab, 0, 0)
    wsec_load(w_in_ab, 512, 1)
    for tb in range(NB):
        sl = tb % 2
        ld("sp", xstg[sl][:, :], x_p[tb * 128:(tb + 1) * 128, :], ch_x[sl], w=[b_xstg[sl]])
        k.op("act", lambda e, sl=sl: e.copy(out=xbt[sl], in_=xstg[sl]), r=[b_xstg[sl]], w=[b_xbt[sl]])
        pb = tb % 2
        pv = PS[pb][:, :].bitcast(BF16).rearrange("p (c t) -> p c t", c=8)

        def tr(e, sl=sl, pv=pv):
            ins = None
            for c in range(8):
                ins = e.transpose(out=pv[:, c, :], in_=xbt[sl][:, c * 128:(c + 1) * 128], identity=ident)
            return ins
        k.op("pe", tr, r=[b_xbt[sl]], w=[psb[pb]], deps=CD)
        k.op("dve", lambda e, tb=tb, pv=pv: e.tensor_copy(out=XT[:, :, tb * 128:(tb + 1) * 128], in_=pv),
             r=[psb[pb]], w=[b_XT[tb]])
    k.op("act", lambda e: e.copy(out=xs_b, in_=xs_f), r=[b_xs], w=[b_xs])
    pvs = PS[2][:, 0:16].bitcast(BF16).rearrange("p (c t) -> p c t", c=8)

    def trs(e):
        ins = None
        for c in range(8):
            ins = e.transpose(out=pvs[:, c, :], in_=xs_b[:, c * 128:(c + 1) * 128], identity=ident[0:4, 0:4])
        return ins
    k.op("pe", trs, r=[b_xs], w=[psb[2]], deps=CD)
    k.op("dve", lambda e: e.tensor_copy(out=XsT, in_=pvs), r=[psb[2]], w=[b_XsT])

    if STOP == 'A0':
        k.emit([(k.sems[e_], k.cnt[e_]) for e_ in k.ENGS if k.cnt[e_] > 0])
        for cm in reversed(cms):
            cm.__exit__(None, None, None)
        return nc
    bank_rr = [0]

    def nextbank(lo=0, hi=4):
        b = lo + bank_rr[0] % (hi - lo)
        bank_rr[0] += 1
        return b

    def proj_fm(wslot, f, tc, xt, b_x, ncols=512):
        pb = nextbank()
        W_ = Wsec[wslot]

        def fn(e):
            ins = None
            for c in range(8):
                ins = e.matmul(PS[pb][:, 0:ncols], lhsT=W_[:, c, f * 128:(f + 1) * 128],
                               rhs=xt(c), start=(c == 0), stop=(c == 7))
            return ins
        k.op("pe", fn, r=[b_Wsec[wslot]] + b_x, w=[psb[pb]])
        return pb

    def proj_tm(wslot, lhs, b_l, m, ncol0=0, ncols=512):
        pb = nextbank()
        W_ = Wsec[wslot]

        def fn(e):
            ins = None
            for c in range(8):
                ins = e.matmul(PS[pb][0:m, 0:ncols], lhsT=lhs(c), rhs=W_[:, c, ncol0:ncol0 + ncols],
                               start=(c == 0), stop=(c == 7))
            return ins
        k.op("pe", fn, r=[b_Wsec[wslot]] + b_l, w=[psb[pb]])
        return pb

    def xt_tc(tc):
        return (lambda c: XT[:, c, tc * 512:(tc + 1) * 512]), b_XT[tc * 4:(tc + 1) * 4]

    SEC = {"q": 0, "k": 512, "v": 1024, "ga": 1536, "ub": 2048, "gb": 2560}
    order = ["q", "k", "v", "ga", "gb", "ub"]
    pss_of = {"q": 0}
    for si, sec in enumerate(order):
        slot = si % 2
        if sec in ("q", "k", "v") and 'sproj' not in SKIP:
            pb = proj_tm(slot, lambda c: XsT[:, c, :], [b_XsT], 4)
        if 'sproj' in SKIP:
            pass
        elif sec in pss_of:
            j = pss_of[sec]
            k.op("act", lambda e, pb=pb, j=j: e.copy(out=ps_s[j], in_=PS[pb][0:4, :]), r=[psb[pb]], w=[b_pss[j]])
        elif sec in ("k", "v"):
            j = 0 if sec == "k" else 1
            k.op("act", lambda e, pb=pb, j=j: e.copy(out=kvs[j], in_=PS[pb][0:4, :]), r=[psb[pb]], w=[b_kvs[j]])
            if 'kvs' not in SKIP:
                k.dma(OQ, out_ch, lambda e, j=j: e.dma_start(out=(k_s if j == 0 else v_s), in_=kvs[j]), r=[b_kvs[j]])
        if sec == "q":
            for f in range(4):
                for tc in range(NTC):
                    xt, bx = xt_tc(tc)
                    pb = proj_fm(slot, f, tc, xt, bx)
                    k.op("act", lambda e, pb=pb, f=f, tc=tc: e.activation(
                        out=QT[:, f, tc * 512:(tc + 1) * 512], in_=PS[pb][:, :], func=AF.Copy, scale=0.125),
                        r=[psb[pb]], w=[b_QT])
        elif sec == "k":
            for f in range(4):
                for tc in range(NTC):
                    xt, bx = xt_tc(tc)
                    pb = proj_fm(slot, f, tc, xt, bx)
                    k.op("dve", lambda e, pb=pb, f=f, tc=tc: e.tensor_copy(
                        out=KT[:, f, tc * 512:(tc + 1) * 512], in_=PS[pb][:, :]), r=[psb[pb]], w=[b_KT])
        if sec in ("k", "v"):
            for tb in range(NB):
                pb = proj_tm(slot, lambda c, tb=tb: XT[:, c, tb * 128:(tb + 1) * 128], [b_XT[tb]], 128)
                sl = tb % 2
                k.op("act", lambda e, pb=pb, sl=sl: e.copy(out=kst[sl], in_=PS[pb][:, :]), r=[psb[pb]], w=[b_kst[sl]])
                if sec == "v":
                    k.op("dve", lambda e, sl=sl, tb=tb: e.tensor_copy(out=V[:, tb, :], in_=kst[sl]),
                         r=[b_kst[sl]], w=[b_V])
                dst = k_p if sec == "k" else v_p
                if 'kst' not in SKIP:
                    k.dma(OQ, ch_kst[sl], lambda e, dst=dst, tb=tb, sl=sl: e.dma_start(
                        out=dst[tb * 128:(tb + 1) * 128, :], in_=kst[sl]), r=[b_kst[sl]])
        elif sec in ("ga", "gb"):
            c0 = 0 if sec == "ga" else 4
            for f in range(4):
                for tc in range(NTC):
                    xt, bx = xt_tc(tc)
                    pb = proj_fm(slot, f, tc, xt, bx)
                    k.op("act", lambda e, pb=pb, f=f, tc=tc, c0=c0: e.activation(
                        out=H[:, c0 + f, tc * 512:(tc + 1) * 512], in_=PS[pb][:, :], func=AF.Silu),
                        r=[psb[pb]], w=[b_H[c0 + f][tc]])
        elif sec == "ub":
            pb = proj_tm(slot, lambda c: XT[:, c, 15 * 128:16 * 128], [b_XT[15]], 128)
            k.op("act", lambda e, pb=pb: e.copy(out=kst[0], in_=PS[pb][:, :]), r=[psb[pb]], w=[b_kst[0]])
            k.dma("sp", ch_kst[0], lambda e: e.dma_start(out=pool_p, in_=kst[0][113:128, :]), r=[b_kst[0]])
            for f in range(4):
                wdw = 2 ** (f + 1)
                for tc in range(NTC):
                    xt, bx = xt_tc(tc)
                    pb = proj_fm(slot, f, tc, xt, bx)
                    if tc == 0:
                        k.op("pool", lambda e: e.memset(Ubuf[:, 0:16], 0.0), w=[b_U])
                    else:
                        k.op("pool", lambda e: e.tensor_copy(out=tmp16, in_=Ubuf[:, 512:528]), r=[b_U], w=[b_Sa])
                        k.op("pool", lambda e: e.tensor_copy(out=Ubuf[:, 0:16], in_=tmp16), r=[b_Sa], w=[b_U])
                    k.op("dve", lambda e, pb=pb: e.tensor_copy(out=Ubuf[:, 16:528], in_=PS[pb][:, :]),
                         r=[psb[pb]], w=[b_U])
                    src, bsrc = Ubuf, b_U
                    dsts = [(Sa, b_Sa), (Sb_, b_Sb), (Sa, b_Sa), (Sb_, b_Sb)]
                    sh = 1
                    for lv in range(f + 1):
                        dst, bdst = dsts[lv]
                        lo = 2 * sh - 1
                        k.op("dve", lambda e, src=src, dst=dst, lo=lo, sh=sh: e.tensor_tensor(
                            out=dst[:, lo:528], in0=src[:, lo:528], in1=src[:, lo - sh:528 - sh], op=ALU.add),
                            r=[bsrc], w=[bdst])
                        src, bsrc = dst, bdst
                        sh *= 2
                    k.op("dve", lambda e, src=src, wdw=wdw: e.scalar_tensor_tensor(
                        out=Dd, in0=src[:, 16:528], scalar=1.0 / wdw, in1=Ubuf[:, 16:528],
                        op0=ALU.mult, op1=ALU.subtract), r=[bsrc, b_U], w=[b_Dd])
                    if tc == 0:
                        k.op("dve", lambda e, src=src, f=f: e.tensor_tensor(
                            out=tmp16, in0=src[:, 16:32], in1=rct[:, f, :], op=ALU.mult), r=[bsrc], w=[b_Sa, b_Sb],
                            deps=CD)
                        k.op("dve", lambda e: e.tensor_tensor(
                            out=Dd[:, 0:16], in0=tmp16, in1=Ubuf[:, 16:32], op=ALU.subtract),
                            r=[b_Sa, b_Sb, b_U], w=[b_Dd])
                    pb2 = nextbank()
                    k.op("pe", lambda e, pb2=pb2, f=f: e.matmul(PS[pb2][:, :], lhsT=wpool_bf[:, f, :], rhs=Dd,
                                                                 start=True, stop=True),
                         r=[b_Dd], w=[psb[pb2]], deps=CD)
                    k.op("dve", lambda e, pb2=pb2, f=f, tc=tc: e.scalar_tensor_tensor(
                        out=H[:, 4 + f, tc * 512:(tc + 1) * 512], in0=PS[pb2][:, :], scalar=pscale[:, f:f + 1],
                        in1=H[:, 4 + f, tc * 512:(tc + 1) * 512], op0=ALU.mult, op1=ALU.mult),
                        r=[psb[pb2]], w=[b_H[4 + f][tc]], deps=CD)
        if si + 2 < len(order):
            wsec_load(w_in_ab, SEC[order[si + 2]], slot)
        if STOP == 'A%d' % (si + 1):
            k.emit([(k.sems[e_], k.cnt[e_]) for e_ in k.ENGS if k.cnt[e_] > 0] + [(c.sem, c.n) for c in ch_kst + ch_W + [out_ch]])
            for cm in reversed(cms):
                cm.__exit__(None, None, None)
            return nc

    def fence(bufs):
        out = []
        for b in bufs:
            out.extend(b.all_tokens())
        return [t for t in out if t is not None]

    f_RW2 = fence(b_XT + [b_XsT])
    f_RW1 = fence(b_Wsec)
    f_RCa = fence(b_xstg + b_xbt + [b_U, b_Sa, b_Sb, b_Dd, b_xs] + b_kst + b_kvs)

    if STOP == 'A':
        k.emit([(out_ch.sem, out_ch.n)] + [(c.sem, c.n) for c in ch_kst])
        for cm in reversed(cms):
            cm.__exit__(None, None, None)
        return nc
    Wout0 = RW1[:, :].rearrange("p (c n) -> p c n", c=8)
    b_Wout0 = Buf("Wout0", init=f_RW1)
    ch_wo0 = k.chan("wo0")
    k.dma("pool", ch_wo0, lambda e: e.dma_start(out=Wout0, in_=w_out_ab.rearrange("(c p) n -> p c n", p=128)),
          w=[b_Wout0])

    TO = 24576
    e_t = [f32v(RQ, TO, 512), f32v(RQ, TO + 1024, 512)]
    sp_t = [RQ[:, TO + 2048:TO + 2560], RQ[:, TO + 2560:TO + 3072]]
    wt_t = [RQ[:, TO + 3072:TO + 3584], RQ[:, TO + 3584:TO + 4096]]
    R_t = f32v(RQ, TO + 4096, 512)
    Rb_t = [RQ[:, TO + 5120:TO + 5632], RQ[:, TO + 5632:TO + 6144]]
    b_e = [Buf("e0"), Buf("e1")]
    b_sp = [Buf("sp0"), Buf("sp1")]
    b_wt = [Buf("wt0"), Buf("wt1")]
    b_R = Buf("R")
    b_Rb = [Buf("Rb0"), Buf("Rb1")]
    ZA = [0, 1]
    OTB = [2, 3]

    NSL = 8
    pg = [RW2[:, 512 * i:512 * (i + 1)] for i in range(NSL)]
    b_pg = [Buf("pg%d" % i, init=f_RW2) for i in range(NSL)]
    ch_pg = [k.chan("pg%d" % i) for i in range(NSL)]
    qb = RW2[:, 4096:6144].rearrange("p (b n) -> p b n", b=4)
    b_qb = Buf("qb", init=f_RW2)
    zs = f32v(RW2, 6144, 2048).rearrange("p (b n) -> p b n", b=4)
    b_zs = [Buf("zs%d" % b, init=f_RW2) for b in range(4)]
    prod = [RW2[:, 10240:10752], RW2[:, 10752:11264]]
    b_prod = [Buf("prod0", init=f_RW2), Buf("prod1", init=f_RW2)]
    wsb = RW2[:, 11264:13312].rearrange("p (b n) -> p b n", b=4)
    b_wsb = [Buf("wsb%d" % b, init=f_RW2) for b in range(4)]
    se = f32v(RW2, 13312, 512)
    ssp = f32v(RW2, 14336, 512)
    scs = f32v(RW2, 15360, 512)
    b_se = Buf("se", init=f_RW2)
    b_ssp = Buf("ssp", init=f_RW2)
    b_scs = Buf("scs", init=f_RW2)
    sbb512 = f32v(RC, 0, 512)
    b_sbb512 = Buf("sbb512", init=f_RCa)
    scs2 = f32v(RC, 1024, 512)
    b_scs2 = Buf("scs2", init=f_RCa)
    osb = f32v(RC, 2048, 512)
    b_osb = Buf("osb", init=f_RCa)
    idxf = f32v(RC, 3072, 256)
    iof = f32v(RC, 3584, 1)
    b_idx = Buf("idx", init=f_RCa)
    oas = SM[0:4, 512:1024]
    b_oas = Buf("oas")
    ch_oas = k.chan("oas")
    SPS = 6
    SPO = 7

    def build_sel(b):
        sel = scs2[0:4, 0:128]
        k.op("pool", lambda e: e.memset(sel, 0.125), w=[b_scs2])
        k.op("pool", lambda e: e.affine_select(out=sel, in_=sel, pattern=[[0, 128]], compare_op=ALU.is_equal,
                                               fill=0.0, base=-b, channel_multiplier=1), w=[b_scs2])
        return sel

    def sample_attention():
        k.dma("sp", ch_c, lambda e: e.dma_start(out=sbb512, in_=sbb512_in), w=[b_sbb512])
        k.op("pool", lambda e: e.iota(iof, pattern=[[0, 1]], base=0, channel_multiplier=1,
                                      allow_small_or_imprecise_dtypes=True), w=[b_idx])
        k.op("dve", lambda e: e.tensor_copy(out=idxf, in_=idx[:, :]), w=[b_idx], deps=CD)
        k.op("dve", lambda e: e.tensor_scalar(out=idxf, in0=idxf, scalar1=128.0, scalar2=iof[:, 0:1],
                                              op0=ALU.mult, op1=ALU.add), w=[b_idx])
        k.op("dve", lambda e: e.tensor_copy(out=idx[:, :], in_=idxf), w=[b_idx])
        yield
        gi = [0]

        def gather(cache, col):
            sl = gi[0] % NSL
            gi[0] += 1
            k.dma("pool", ch_pg[sl], lambda e, sl=sl: e.indirect_dma_start(
                out=pg[sl], out_offset=None, in_=cache,
                in_offset=bass.IndirectOffsetOnAxis(ap=idx[:, col:col + 1], axis=0)),
                r=[b_idx], w=[b_pg[sl]])
            return sl

        for b in range(4):
            sel = build_sel(b)
            k.op("pe", lambda e: e.matmul(PS[SPS][:, :], lhsT=sel, rhs=ps_s[0], start=True, stop=True),
                 r=[b_scs2, b_pss[0]], w=[psb[SPS]])
            k.op("dve", lambda e, b=b: e.tensor_copy(out=qb[:, b, :], in_=PS[SPS][:, :]), r=[psb[SPS]], w=[b_qb])
            yield
            pend = []
            PRE = NSL - 2
            for j in range(NPAGE + PRE):
                if j < NPAGE:
                    pend.append((j, gather(cache_k, b * NPAGE + j)))
                if j >= PRE:
                    jj, sl = pend.pop(0)
                    pr = jj % 2
                    k.op("dve", lambda e, sl=sl, pr=pr, b=b: e.tensor_tensor(
                        out=prod[pr], in0=pg[sl], in1=qb[:, b, :], op=ALU.mult),
                        r=[b_pg[sl], b_qb], w=[b_prod[pr]])
                    k.op("dve", lambda e, pr=pr, b=b, jj=jj: e.tensor_reduce(
                        out=zs[:, b, jj * 8:(jj + 1) * 8], in_=prod[pr].rearrange("p (h d) -> p h d", h=8),
                        axis=AX.X, op=ALU.add), r=[b_prod[pr]], w=[b_zs[b]])
                    yield
            k.op("dve", lambda e, b=b: e.tensor_tensor(out=zs[:, b, :], in0=zs[:, b, :], in1=sbb512, op=ALU.add),
                 r=[b_sbb512], w=[b_zs[b]])
            k.op("act", lambda e, b=b: e.activation(out=se, in_=zs[:, b, :], func=AF.Exp), r=[b_zs[b]], w=[b_se])
            k.op("act", lambda e: e.activation(out=ssp, in_=se, func=AF.Ln, bias=1.0), r=[b_se], w=[b_ssp])
            yield
            k.op("pe", lambda e: e.matmul(PS[SPS][:, :], lhsT=onesf, rhs=ssp, start=True, stop=True),
                 r=[b_ssp], w=[psb[SPS]], deps=CD)
            k.op("dve", lambda e: e.tensor_copy(out=scs, in_=PS[SPS][:, :]), r=[psb[SPS]], w=[b_scs])
            k.op("pool", lambda e: e.memset(scs2[:, 504:512], 0.0), w=[b_scs2])
            k.op("dve", lambda e: e.tensor_copy(out=scs2[:, 0:504], in_=scs[:, 8:512]), r=[b_scs], w=[b_scs2])
            a, ba, c_, bc = scs2, b_scs2, scs, b_scs
            sh = 1
            while sh < NPAGE:
                n = (NPAGE - sh) * 8
                k.op("dve", lambda e, a=a, c_=c_, n=n, sh=sh: e.tensor_tensor(
                    out=c_[:, 0:n], in0=a[:, 0:n], in1=a[:, sh * 8:sh * 8 + n], op=ALU.add), r=[ba], w=[bc])
                k.op("dve", lambda e, a=a, c_=c_, n=n: e.tensor_copy(out=c_[:, n:512], in_=a[:, n:512]), r=[ba], w=[bc])
                a, ba, c_, bc = c_, bc, a, ba
                sh *= 2
            yield
            k.op("pe", lambda e: e.matmul(PS[SPS][:, :], lhsT=trif, rhs=ssp, start=True, stop=True),
                 r=[b_ssp], w=[psb[SPS]], deps=CD)
            k.op("dve", lambda e, a=a: e.tensor_tensor(out=se, in0=PS[SPS][:, :], in1=a, op=ALU.add),
                 r=[psb[SPS], ba], w=[b_se])
            k.op("dve", lambda e, b=b: e.tensor_tensor(out=se, in0=zs[:, b, :], in1=se, op=ALU.subtract),
                 r=[b_zs[b]], w=[b_se])
            k.op("act", lambda e, b=b: e.activation(out=wsb[:, b, :], in_=se, func=AF.Exp), r=[b_se], w=[b_wsb[b]])
            yield
            pend = []
            for j in range(NPAGE + PRE):
                if j < NPAGE:
                    pend.append((j, gather(cache_v, b * NPAGE + j)))
                if j >= PRE:
                    jj, sl = pend.pop(0)
                    k.op("pe", lambda e, sl=sl, b=b, jj=jj: e.matmul(
                        PS[SPO][0:8, :], lhsT=wsb[:, b, jj * 8:(jj + 1) * 8], rhs=pg[sl],
                        start=(jj == 0), stop=(jj == NPAGE - 1)), r=[b_pg[sl], b_wsb[b]], w=[psb[SPO]])
                    yield
            k.op("dve", lambda e: e.tensor_copy(out=osb[0:8, :], in_=PS[SPO][0:8, :]), r=[psb[SPO]], w=[b_osb])
            for h in range(8):
                k.dma("sp", ch_oas, lambda e, h=h, b=b: e.dma_start(
                    out=oas[b:b + 1, h * 64:(h + 1) * 64], in_=osb[h:h + 1, h * 64:(h + 1) * 64]),
                    r=[b_osb], w=[b_oas])
            yield

    sgen = sample_attention()

    def sample_tick(n=1):
        for _ in range(n):
            try:
                next(sgen)
            except StopIteration:
                return False
        return True

    blocks = []
    for i in range(4):
        for qc in range(NTC):
            for hh in range(2):
                kbs = list(range(4 * qc + 3, -1, -1))
                for n_, kb in enumerate(kbs):
                    blocks.append(dict(i=i, qc=qc, hh=hh, kb=kb, first=(n_ == 0), last=(kb == 0),
                                       gfirst=(hh == 0 and n_ == 0), glast=(hh == 1 and kb == 0)))

    def stageA(n, bl):
        s = n % 2
        i, qc, hh, kb = bl["i"], bl["qc"], bl["hh"], bl["kb"]
        p0 = hh * 64
        h = 2 * i + hh
        if bl["first"]:
            k.op("pool", lambda e: e.memset(R_t, 0.0), w=[b_R])
        k.op("pe", lambda e: e.matmul(PS[ZA[s]][:, :], lhsT=KT[p0:p0 + 64, i, kb * 128:(kb + 1) * 128],
                                      rhs=QT[p0:p0 + 64, i, qc * 512:(qc + 1) * 512], start=True, stop=True),
             r=[b_KT, b_QT], w=[psb[ZA[s]]])
        k.op("act", lambda e: e.activation(out=e_t[s], in_=PS[ZA[s]][:, :], func=AF.Exp, bias=sbb[:, h:h + 1]),
             r=[psb[ZA[s]]], w=[b_e[s]], deps=CD)
        k.op("act", lambda e: e.activation(out=sp_t[s], in_=e_t[s], func=AF.Ln, bias=1.0), r=[b_e[s]], w=[b_sp[s]])
        di = kb - 4 * qc
        if di >= 0:
            k.op("pool", lambda e: e.affine_select(out=sp_t[s], in_=sp_t[s], pattern=[[1, 512]],
                                                   compare_op=ALU.is_gt, fill=0.0, base=-128 * di,
                                                   channel_multiplier=-1), w=[b_sp[s]])

    def stageB(n, bl):
        s = n % 2
        i, qc, hh, kb = bl["i"], bl["qc"], bl["hh"], bl["kb"]
        h = 2 * i + hh
        first = bl["first"]

        p0 = hh * 64

        def fn(e):
            e.matmul(PS[ZA[s]][:, :], lhsT=KT[p0:p0 + 64, i, kb * 128:(kb + 1) * 128],
                     rhs=QT[p0:p0 + 64, i, qc * 512:(qc + 1) * 512], start=True, stop=False)
            ins = e.matmul(PS[ZA[s]][:, :], lhsT=trineg, rhs=sp_t[s], start=False, stop=first)
            if not first:
                ins = e.matmul(PS[ZA[s]][:, :], lhsT=onesneg, rhs=Rb_t[s], start=False, stop=True)
            return ins
        k.op("pe", fn, r=[b_sp[s]] + ([] if first else [b_Rb[s]]), w=[psb[ZA[s]]], deps=CD)
        k.op("act", lambda e: e.activation(out=wt_t[s], in_=PS[ZA[s]][:, :], func=AF.Exp, bias=sbb[:, h:h + 1]),
             r=[psb[ZA[s]]], w=[b_wt[s]])
        di = kb - 4 * qc
        if di >= 0:
            k.op("pool", lambda e: e.affine_select(out=wt_t[s], in_=wt_t[s], pattern=[[1, 512]],
                                                   compare_op=ALU.is_gt, fill=0.0, base=-128 * di,
                                                   channel_multiplier=-1), w=[b_wt[s]])
        ot = OTB[(i * NTC + qc) % 2]
        k.op("pe", lambda e: e.matmul(PS[ot][hh * 64:hh * 64 + 64, :], lhsT=V[:, kb, h * 64:(h + 1) * 64],
                                      rhs=wt_t[s], start=bl["first"], stop=bl["last"]),
             r=[b_wt[s], b_V], w=[psb[ot]])
        if not bl["last"]:
            k.op("dve", lambda e: e.tensor_tensor(out=R_t, in0=R_t, in1=sp_t[s], op=ALU.add), r=[b_sp[s]], w=[b_R])
            k.op("pool", lambda e: e.tensor_copy(out=Rb_t[(n + 1) % 2], in_=R_t), r=[b_R], w=[b_Rb[(n + 1) % 2]])
        if bl["glast"]:
            k.op("dve", lambda e: e.tensor_tensor(out=H[:, i, qc * 512:(qc + 1) * 512], in0=PS[ot][:, :],
                                                  in1=H[:, i, qc * 512:(qc + 1) * 512], op=ALU.mult),
                 r=[psb[ot]], w=[b_H[i][qc]])

    stageA(0, blocks[0])
    for n, bl in enumerate(blocks):
        if n + 1 < len(blocks):
            stageA(n + 1, blocks[n + 1])
        stageB(n, bl)
        sample_tick(2)
    while sample_tick(1):
        pass

    if STOP == 'B':
        k.emit([(out_ch.sem, out_ch.n)] + [(c.sem, c.n) for c in ch_kst] + [(ch_oas.sem, ch_oas.n)] + [(k.sems[e_], k.cnt[e_]) for e_ in k.ENGS])
        for cm in reversed(cms):
            cm.__exit__(None, None, None)
        return nc
    f_RQ = fence([b_QT, b_KT, b_V, b_R] + b_e + b_sp + b_wt + b_Rb)
    f_RW2b = fence(b_pg + [b_qb] + b_zs + b_prod + b_wsb + [b_se, b_ssp, b_scs])
    f_RCb = fence([b_sbb512, b_scs2, b_osb, b_idx])

    if DEBUG:
        k.dma("sp", out_ch, lambda e: e.dma_start(out=dbg_h, in_=RH[:, :]), r=[b for row in b_H for b in row])

    WoutC = RW2[:, 0:8192].rearrange("p (c n) -> p c n", c=8)
    b_WoutC = Buf("WoutC", init=f_RW2b)
    ch_woc = k.chan("woc")
    k.dma("pool", ch_woc, lambda e: e.dma_start(out=WoutC, in_=w_out_c.rearrange("(c p) n -> p c n", p=128)),
          w=[b_WoutC])
    Wsec[0] = RW2[:, 8192:12288].rearrange("p (c n) -> p c n", c=8)
    Wsec[1] = RW2[:, 12288:16384].rearrange("p (c n) -> p c n", c=8)
    b_Wsec[0] = Buf("WsecC0", init=f_RW2b)
    b_Wsec[1] = Buf("WsecC1", init=f_RW2b)
    ch_W[0] = k.chan("WC0")
    ch_W[1] = k.chan("WC1")

    X1 = f32v(RQ, 0, 4096).rearrange("p (b n) -> p b n", b=4)
    X1T = RQ[:, 8192:12288].rearrange("p (c t) -> p c t", c=8)
    HC = RQ[:, 12288:12288 + 8 * 544].rearrange("p (c t) -> p c t", c=8)
    G2 = RQ[:, 16640:20736].rearrange("p (c t) -> p c t", c=8)
    CC = f32v(RQ, 20736, 4096).rearrange("p (c t) -> p c t", c=8)
    b_X1 = [Buf("X1_%d" % i, init=f_RQ) for i in range(4)]
    b_X1T = Buf("X1T", init=f_RQ)
    b_HC = [Buf("HC%d" % i, init=f_RQ) for i in range(8)]
    b_G2 = [Buf("G2%d" % i, init=f_RQ) for i in range(8)]
    b_CC = [Buf("CC%d" % i, init=f_RQ) for i in range(8)]
    DG = [RC[:, 0:3968].rearrange("p (t n) -> p t n", t=31), RC[:, 3968:7936].rearrange("p (t n) -> p t n", t=31)]
    b_DG = [Buf("DG0", init=f_RCb), Buf("DG1", init=f_RCb)]
    ystg = [f32v(RC, 7936, 1024), f32v(RC, 9984, 1024)]
    b_ystg = [Buf("ystg0", init=f_RCb), Buf("ystg1", init=f_RCb)]
    ch_y = [k.chan("y0"), k.chan("y1")]
    csq = [f32v(RC, 12032, 512), f32v(RC, 13056, 512)]
    b_csq = [Buf("csq0", init=f_RCb), Buf("csq1", init=f_RCb)]
    xst = f32v(RC, 14080, 1024)
    b_xst = Buf("xst", init=f_RCb)
    ch_xst = k.chan("xst")
    lnt = sb("lnt", [128, 64], F32)
    b_lnt = Buf("lnt")
    xbt2 = RQ[:, 28928:29952]
    b_xbt2 = Buf("xbt2", init=f_RQ)
    mean_t = f32v(RQ, 29952, 512)
    rstd_t = f32v(RQ, 30976, 512)
    b_mean = Buf("mean", init=f_RQ)
    b_rstd = Buf("rstd")

    def layer_norm_tm(buf, bbuf, m, gi, bi):
        st = lnt[0:m, 0:12].rearrange("p (a s) -> p a s", a=2)
        mv = lnt[0:m, 12:14]
        rs = lnt[0:m, 14:15]
        nb = lnt[0:m, 15:16]
        k.op("dve", lambda e: e.bn_stats(out=st[:, 0, :], in_=buf[:, 0:512]), r=[bbuf], w=[b_lnt])
        k.op("dve", lambda e: e.bn_stats(out=st[:, 1, :], in_=buf[:, 512:1024]), r=[bbuf], w=[b_lnt])
        k.op("dve", lambda e: e.bn_aggr(out=mv, in_=lnt[0:m, 0:12]), w=[b_lnt])
        k.op("act", lambda e: e.activation(out=rs, in_=mv[:, 1:2], func=AF.Sqrt, bias=epsb[0:m, :]), w=[b_lnt], deps=CD)
        k.op("dve", lambda e: e.reciprocal(out=rs, in_=rs), w=[b_lnt])
        k.op("dve", lambda e: e.scalar_tensor_tensor(out=nb, in0=mv[:, 0:1], scalar=-1.0, in1=rs,
                                                     op0=ALU.mult, op1=ALU.mult), w=[b_lnt])
        k.op("act", lambda e: e.activation(out=buf, in_=buf, func=AF.Identity, bias=nb, scale=rs),
             r=[b_lnt], w=[bbuf])
        k.op("pool", lambda e: e.tensor_tensor(out=buf, in0=buf, in1=LNP[0:m, gi, :], op=ALU.mult), w=[bbuf], deps=CD)
        k.op("pool", lambda e: e.tensor_tensor(out=buf, in0=buf, in1=LNP[0:m, bi, :], op=ALU.add), w=[bbuf])

    def outproj_tm(lhs, b_l, wout, b_wout, m):
        banks = []
        for half in range(2):
            pb = nextbank(0, 4)

            def fn(e, pb=pb, half=half):
                ins = None
                for c in range(8):
                    ins = e.matmul(PS[pb][0:m, :], lhsT=lhs(c), rhs=wout[:, c, half * 512:(half + 1) * 512],
                                   start=(c == 0), stop=(c == 7))
                return ins
            k.op("pe", fn, r=b_l + [b_wout], w=[psb[pb]])
            banks.append(pb)
        return banks

    def transpose_to_fm(src, bsrc, m, dst, bdst):
        k.op("act", lambda e: e.copy(out=xbt2[0:m, :], in_=src), r=[bsrc], w=[b_xbt2])
        pb = nextbank(4, 6)
        pv = PS[pb][:, :].bitcast(BF16).rearrange("p (c t) -> p c t", c=8)

        def tr(e):
            ins = None
            for c in range(8):
                ins = e.transpose(out=pv[:, c, 0:m], in_=xbt2[0:m, c * 128:(c + 1) * 128], identity=ident[0:m, 0:m])
            return ins
        k.op("pe", tr, r=[b_xbt2], w=[psb[pb]], deps=CD)
        k.op("dve", lambda e: e.tensor_copy(out=dst, in_=pv[:, :, 0:m]), r=[psb[pb]], w=bdst)

    def load_csec(s, slot):
        if s < 4:
            wsec_load(w_in_c, 256 * s, slot, ncol=256, dcol=0)
            wsec_load(w_in_c, 1024 + 256 * s, slot, ncol=256, dcol=256)
        else:
            wsec_load(w_in_c, 2048 + 512 * (s - 4), slot)

    S1B, S2B = 6, 7

    def conv_ln_gate(ncols, hist_ready_bufs):
        for f in range(8):
            ds = f % 2
            for t in range(31):
                k.op("pool", lambda e, ds=ds, t=t, f=f: e.tensor_scalar(
                    out=DG[ds][:, t, :], in0=ident, scalar1=wdwT[:, f, t:t + 1], scalar2=None, op0=ALU.mult),
                    w=[b_DG[ds]], deps=CD)
            pb = nextbank(4, 6)

            def cv(e, ds=ds, f=f, pb=pb):
                ins = None
                for t in range(31):
                    ins = e.matmul(PS[pb][:, 0:ncols], lhsT=DG[ds][:, t, :], rhs=HC[:, f, 2 + t:2 + t + ncols],
                                   start=(t == 0), stop=(t == 30))
                return ins
            k.op("pe", cv, r=[b_DG[ds], b_HC[f]], w=[psb[pb]])
            k.op("act", lambda e, f=f, pb=pb: e.activation(out=CC[:, f, 0:ncols], in_=PS[pb][:, 0:ncols],
                                                           func=AF.Identity, bias=cvec[:, 2, f:f + 1]),
                 r=[psb[pb]], w=[b_CC[f]], deps=CD)
            k.op("act", lambda e, f=f, ds=ds: e.activation(out=csq[ds][:, 0:ncols], in_=CC[:, f, 0:ncols],
                                                           func=AF.Square), r=[b_CC[f]], w=[b_csq[ds]])

            def st(e, f=f, ds=ds):
                e.matmul(PS[S1B][:, 0:ncols], lhsT=onesf, rhs=CC[:, f, 0:ncols], start=(f == 0), stop=(f == 7))
                return e.matmul(PS[S2B][:, 0:ncols], lhsT=onesf, rhs=csq[ds][:, 0:ncols], start=(f == 0), stop=(f == 7))
            k.op("pe", st, r=[b_CC[f], b_csq[ds]], w=[psb[S1B], psb[S2B]], deps=CD)
        mt = mean_t[:, 0:ncols]
        rt = rstd_t[:, 0:ncols]
        k.op("dve", lambda e: e.tensor_scalar(out=mt, in0=PS[S1B][:, 0:ncols], scalar1=1.0 / 1024, scalar2=None,
                                              op0=ALU.mult), r=[psb[S1B]], w=[b_mean])
        k.op("dve", lambda e: e.tensor_tensor(out=rt, in0=mt, in1=mt, op=ALU.mult), r=[b_mean], w=[b_rstd])
        k.op("dve", lambda e: e.scalar_tensor_tensor(out=rt, in0=PS[S2B][:, 0:ncols], scalar=1.0 / 1024, in1=rt,
                                                     op0=ALU.mult, op1=ALU.subtract), r=[psb[S2B]], w=[b_rstd])
        k.op("act", lambda e: e.activation(out=rt, in_=rt, func=AF.Sqrt, bias=epsb), w=[b_rstd], deps=CD)
        k.op("dve", lambda e: e.reciprocal(out=rt, in_=rt), w=[b_rstd])
        for f in range(8):
            ds = f % 2
            k.op("dve", lambda e, f=f: e.tensor_tensor(out=CC[:, f, 0:ncols], in0=CC[:, f, 0:ncols], in1=mt,
                                                       op=ALU.subtract), r=[b_mean], w=[b_CC[f]])
            k.op("pool", lambda e, f=f: e.tensor_tensor(out=CC[:, f, 0:ncols], in0=CC[:, f, 0:ncols], in1=rt,
                                                        op=ALU.mult), r=[b_rstd], w=[b_CC[f]])
            k.op("act", lambda e, f=f, ds=ds: e.activation(out=csq[ds][:, 0:ncols], in_=CC[:, f, 0:ncols],
                                                           func=AF.Silu, bias=cvec[:, 1, f:f + 1],
                                                           scale=cvec[:, 0, f:f + 1]),
                 r=[b_CC[f]], w=[b_csq[ds]], deps=CD)
            k.op("dve", lambda e, f=f, ds=ds: e.tensor_tensor(out=G2[:, f, 0:ncols], in0=csq[ds][:, 0:ncols],
                                                              in1=G2[:, f, 0:ncols], op=ALU.mult),
                 r=[b_csq[ds]], w=[b_G2[f]])

    for f in range(8):
        k.op("pool", lambda e, f=f: e.memset(HC[:, f, 0:32], 0.0), w=[b_HC[f]])

    csec_i = [0]

    def next_csec():
        s = csec_i[0] % 6
        slot = csec_i[0] % 2
        csec_i[0] += 1
        return s, slot

    pre = [next_csec(), next_csec()]
    for s, slot in pre:
        load_csec(s, slot)
    pending_secs = list(pre)

    for tc in range(NTC):
        for bi_ in range(4):
            tb = tc * 4 + bi_
            banks = outproj_tm(lambda c, tb=tb: H[:, c, tb * 128:(tb + 1) * 128], [b_H[c][tc] for c in range(8)],
                               Wout0, b_Wout0, 128)
            k.dma("sp", ch_xst, lambda e, tb=tb: e.dma_start(out=xst, in_=x_p[tb * 128:(tb + 1) * 128, :]), w=[b_xst])
            for half in range(2):
                k.op("dve", lambda e, half=half, bi_=bi_, pb=banks[half]: e.scalar_tensor_tensor(
                    out=X1[:, bi_, half * 512:(half + 1) * 512], in0=xst[:, half * 512:(half + 1) * 512],
                    scalar=ALPHA, in1=PS[pb][:, :], op0=ALU.mult, op1=ALU.add),
                    r=[b_xst, psb[banks[half]]], w=[b_X1[bi_]])
            layer_norm_tm(X1[:, bi_, :], b_X1[bi_], 128, 0, 1)
            transpose_to_fm(X1[:, bi_, :], b_X1[bi_], 128, X1T[:, :, bi_ * 128:(bi_ + 1) * 128], [b_X1T])
        for s6 in range(6):
            s, slot = pending_secs.pop(0)
            assert s == s6
            if s < 4:
                for j in range(2):
                    f = 2 * s + j
                    pa = proj_fm(slot, j, 0, lambda c: X1T[:, c, :], [b_X1T])
                    pg_ = proj_fm(slot, 2 + j, 0, lambda c: X1T[:, c, :], [b_X1T])
                    ds = f % 2
                    k.op("act", lambda e, pg_=pg_, ds=ds: e.activation(out=csq[ds], in_=PS[pg_][:, :], func=AF.Sigmoid),
                         r=[psb[pg_]], w=[b_csq[ds]])
                    k.op("dve", lambda e, pa=pa, ds=ds, f=f: e.tensor_tensor(
                        out=HC[:, f, 32:544], in0=PS[pa][:, :], in1=csq[ds], op=ALU.mult),
                        r=[psb[pa], b_csq[ds]], w=[b_HC[f]])
                if tc == NTC - 1:
                    pb = proj_tm(slot, lambda c: X1T[:, c, 384:512], [b_X1T], 128)
                    k.op("act", lambda e, pb=pb: e.activation(out=ystg[0][:, 256:512], in_=PS[pb][:, 256:512],
                                                              func=AF.Sigmoid), r=[psb[pb]], w=[b_ystg[0]])
                    k.op("dve", lambda e, pb=pb: e.tensor_tensor(out=ystg[0][:, 0:256], in0=PS[pb][:, 0:256],
                                                                 in1=ystg[0][:, 256:512], op=ALU.mult),
                         r=[psb[pb]], w=[b_ystg[0]])
                    k.dma("sp", ch_y[0], lambda e, s=s: e.dma_start(out=conv_p[:, 256 * s:256 * (s + 1)],
                                                                    in_=ystg[0][98:128, 0:256]), r=[b_ystg[0]])
            else:
                for j in range(4):
                    f = 4 * (s - 4) + j
                    pb = proj_fm(slot, j, 0, lambda c: X1T[:, c, :], [b_X1T])
                    k.op("act", lambda e, pb=pb, f=f: e.activation(out=G2[:, f, :], in_=PS[pb][:, :], func=AF.Silu),
                         r=[psb[pb]], w=[b_G2[f]])
            if not (tc == NTC - 1 and s6 >= 4):
                nx = next_csec()
                load_csec(*nx)
                pending_secs.append(nx)
        conv_ln_gate(512, None)
        for f in range(8):
            k.op("pool", lambda e, f=f: e.tensor_copy(out=xbt2[:, f * 32:f * 32 + 30], in_=HC[:, f, 514:544]),
                 r=[b_HC[f]], w=[b_xbt2])
            k.op("pool", lambda e, f=f: e.tensor_copy(out=HC[:, f, 2:32], in_=xbt2[:, f * 32:f * 32 + 30]),
                 r=[b_xbt2], w=[b_HC[f]])
        for bi_ in range(4):
            tb = tc * 4 + bi_
            ys = tb % 2
            banks = outproj_tm(lambda c, bi_=bi_: G2[:, c, bi_ * 128:(bi_ + 1) * 128], b_G2, WoutC, b_WoutC, 128)
            for half in range(2):
                k.op("dve", lambda e, half=half, bi_=bi_, pb=banks[half], ys=ys: e.scalar_tensor_tensor(
                    out=ystg[ys][:, half * 512:(half + 1) * 512], in0=X1[:, bi_, half * 512:(half + 1) * 512],
                    scalar=ALPHA, in1=PS[pb][:, :], op0=ALU.mult, op1=ALU.add),
                    r=[b_X1[bi_], psb[banks[half]]], w=[b_ystg[ys]])
            layer_norm_tm(ystg[ys][:, :], b_ystg[ys], 128, 2, 3)
            k.dma("sp", ch_y[ys], lambda e, tb=tb, ys=ys: e.dma_start(out=y_p[tb * 128:(tb + 1) * 128, :], in_=ystg[ys]),
                  r=[b_ystg[ys]])

    if STOP == 'C':
        k.emit([(out_ch.sem, out_ch.n)] + [(c.sem, c.n) for c in ch_kst + ch_y] + [(k.sems[e_], k.cnt[e_]) for e_ in k.ENGS])
        for cm in reversed(cms):
            cm.__exit__(None, None, None)
        return nc
    def barrier():
        return [(k.sems[e_], k.cnt[e_]) for e_ in k.ENGS if k.cnt[e_] > 0]
    BAR = barrier() + [(c.sem, c.n) for c in ch_y + [ch_xst]]
    RHf = RH[:, :].bitcast(F32)
    RQf = RQ[:, 0:28928].bitcast(F32)
    bump = {"h": 0, "q": 0}

    def talloc(reg, parts, n):
        base = RHf if reg == "h" else RQf
        o = bump[reg]
        bump[reg] += n
        assert bump[reg] <= (8192 if reg == "h" else 14464)
        return base[0:parts, o:o + n]

    def tbuf(name):
        return Buf(name, init=BAR)

    p0s = talloc("h", 4, 1536)
    b_p0s = tbuf("p0s")
    wsec_load(w_in_ab, SEC["ga"], 0)
    wsec_load(w_in_ab, SEC["ub"], 1)
    for n_, sec in enumerate(["ga", "ub", "gb"]):
        slot = n_ % 2
        pb = proj_tm(slot, lambda c: XsT[:, c, :], [b_XsT], 4)
        k.op("act", lambda e, pb=pb, n_=n_: e.copy(out=p0s[:, 512 * n_:512 * (n_ + 1)], in_=PS[pb][0:4, :]),
             r=[psb[pb]], w=[b_p0s])
        if n_ == 0:
            wsec_load(w_in_ab, SEC["gb"], 0)
    ga_s, ub_s, gb_s = p0s[:, 0:512], p0s[:, 512:1024], p0s[:, 1024:1536]
    uext = talloc("q", 64, 512)
    b_uext = tbuf("uext")
    pind = talloc("q", 64, 16)
    ch_sp = k.chan("spin")
    for b in range(4):
        k.dma("sp", ch_sp, lambda e, b=b: e.dma_start(out=uext[16 * b:16 * b + 15, :], in_=st_pool[b]), w=[b_uext])
        k.dma("sp", ch_sp, lambda e, b=b: e.dma_start(out=uext[16 * b + 15:16 * b + 16, :], in_=ub_s[b:b + 1, :]),
              r=[b_p0s], w=[b_uext])
    k.dma("sp", ch_sp, lambda e: e.dma_start(out=pind, in_=pind_in), w=[b_uext])
    ps4 = talloc("q", 4, 512)
    k.dma("sp", ch_sp, lambda e: e.dma_start(out=ps4, in_=ps4_in), w=[b_uext])
    b_uext.w = (ch_sp.sem, ch_sp.n)
    for b in range(4):
        k.dma("sp", out_ch, lambda e, b=b: e.dma_start(out=pool_s[b], in_=uext[16 * b + 1:16 * b + 16, :]), r=[b_uext])
    pb = nextbank(4, 6)

    def pd(e, pb=pb):
        ins = None
        for g in range(4):
            ins = e.matmul(PS[pb][0:4, g * 128:(g + 1) * 128], lhsT=pind[:, 4 * g:4 * g + 4],
                           rhs=uext[:, g * 128:(g + 1) * 128], start=True, stop=True)
        return ins
    k.op("pe", pd, r=[b_uext], w=[psb[pb]])
    dsb = talloc("q", 4, 256).bitcast(BF16)
    b_ds = tbuf("ds")
    k.op("act", lambda e, pb=pb: e.copy(out=dsb, in_=PS[pb][0:4, :]), r=[psb[pb]], w=[b_ds])
    dsT = talloc("q", 128, 8).bitcast(BF16).rearrange("p (c t) -> p c t", c=4)
    b_dsT = tbuf("dsT")
    pb = nextbank(4, 6)
    pvd = PS[pb][:, 0:8].bitcast(BF16).rearrange("p (c t) -> p c t", c=4)

    def trd(e):
        ins = None
        for c in range(4):
            ins = e.transpose(out=pvd[:, c, :], in_=dsb[:, c * 128:(c + 1) * 128], identity=ident[0:4, 0:4])
        return ins
    k.op("pe", trd, r=[b_ds], w=[psb[pb]], deps=CD)
    k.op("dve", lambda e: e.tensor_copy(out=dsT, in_=pvd), r=[psb[pb]], w=[b_dsT])
    pb = nextbank(4, 6)

    def pm(e, pb=pb):
        ins = None
        for g in range(4):
            ins = e.matmul(PS[pb][0:4, g * 128:(g + 1) * 128], lhsT=dsT[:, g, :], rhs=wpool_bf[:, g, :],
                           start=True, stop=True)
        return ins
    k.op("pe", pm, r=[b_dsT], w=[psb[pb]], deps=CD)
    hs = talloc("h", 4, 1024)
    b_hs = tbuf("hs")
    gsl = talloc("h", 4, 1024)
    b_gsl = tbuf("gsl")
    k.op("act", lambda e: e.activation(out=gsl[:, 0:512], in_=ga_s, func=AF.Silu), r=[b_p0s], w=[b_gsl])
    k.op("act", lambda e: e.activation(out=gsl[:, 512:1024], in_=gb_s, func=AF.Silu), r=[b_p0s], w=[b_gsl])
    k.op("dve", lambda e, pb=pb: e.tensor_tensor(out=hs[:, 512:1024], in0=PS[pb][0:4, :], in1=ps4, op=ALU.mult),
         r=[psb[pb], b_uext], w=[b_hs])
    k.op("dve", lambda e: e.tensor_tensor(out=hs[:, 512:1024], in0=hs[:, 512:1024], in1=gsl[:, 512:1024], op=ALU.mult),
         r=[b_gsl], w=[b_hs])
    k.op("dve", lambda e: e.tensor_tensor(out=hs[:, 0:512], in0=oas[:, :], in1=gsl[:, 0:512], op=ALU.mult),
         r=[b_gsl, b_oas], w=[b_hs])
    hsT = sb("hsT", [128, 8, 4], BF16)
    b_hsT = Buf("hsT")
    transpose_to_fm(hs, b_hs, 4, hsT[:, :, :], [b_hsT])
    banks = outproj_tm(lambda c: hsT[:, c, :], [b_hsT], Wout0, b_Wout0, 4)
    x1s = talloc("h", 4, 1024)
    b_x1s = tbuf("x1s")
    xs2 = talloc("h", 4, 1024)
    b_xs2 = tbuf("xs2")
    ch_x2 = k.chan("xs2")
    k.dma("sp", ch_x2, lambda e: e.dma_start(out=xs2, in_=x_s), w=[b_xs2])
    for half in range(2):
        k.op("dve", lambda e, half=half, pb=banks[half]: e.scalar_tensor_tensor(
            out=x1s[:, half * 512:(half + 1) * 512], in0=xs2[:, half * 512:(half + 1) * 512], scalar=ALPHA,
            in1=PS[pb][0:4, :], op0=ALU.mult, op1=ALU.add), r=[b_xs2, psb[banks[half]]], w=[b_x1s])
    layer_norm_tm(x1s, b_x1s, 4, 0, 1)
    x1sT = sb("x1sT", [128, 8, 4], BF16)
    b_x1sT = Buf("x1sT")
    transpose_to_fm(x1s, b_x1s, 4, x1sT[:, :, :], [b_x1sT])
    p1 = talloc("q", 4, 3072)
    b_p1 = tbuf("p1")
    csec_i[0] = 0
    pre = [next_csec(), next_csec()]
    for s, slot in pre:
        load_csec(s, slot)
    pending_secs = list(pre)
    for s6 in range(6):
        s, slot = pending_secs.pop(0)
        pb = proj_tm(slot, lambda c: x1sT[:, c, :], [b_x1sT], 4)
        if s < 4:
            k.op("act", lambda e, pb=pb, s=s: e.copy(out=p1[:, 256 * s:256 * (s + 1)], in_=PS[pb][0:4, 0:256]),
                 r=[psb[pb]], w=[b_p1])
            k.op("act", lambda e, pb=pb, s=s: e.copy(out=p1[:, 1024 + 256 * s:1024 + 256 * (s + 1)],
                                                     in_=PS[pb][0:4, 256:512]), r=[psb[pb]], w=[b_p1])
        else:
            k.op("act", lambda e, pb=pb, s=s: e.copy(out=p1[:, 2048 + 512 * (s - 4):2048 + 512 * (s - 3)],
                                                     in_=PS[pb][0:4, :]), r=[psb[pb]], w=[b_p1])
        if s6 + 2 < 6:
            nx = next_csec()
            load_csec(*nx)
            pending_secs.append(nx)
    hsl = talloc("h", 4, 1024)
    b_hsl = tbuf("hsl")
    k.op("act", lambda e: e.activation(out=hsl, in_=p1[:, 1024:2048], func=AF.Sigmoid), r=[b_p1], w=[b_hsl])
    k.op("dve", lambda e: e.tensor_tensor(out=hsl, in0=hsl, in1=p1[:, 0:1024], op=ALU.mult), r=[b_p1], w=[b_hsl])
    hext = talloc("q", 124, 1024)
    wrep = talloc("q", 124, 1024)
    cv4 = talloc("q", 4, 3072).rearrange("p (a d) -> p a d", a=3)
    b_hext = tbuf("hext")
    b_wrep = tbuf("wrep")
    ch_he = k.chan("hext")
    k.dma("sp", ch_he, lambda e: e.dma_start(out=wrep, in_=wdwrep_in), w=[b_wrep])
    k.dma("sp", ch_he, lambda e: e.dma_start(out=cv4, in_=cvec4_in.rearrange("a p d -> p a d")), w=[b_wrep])
    for b in range(4):
        k.dma("sp", ch_he, lambda e, b=b: e.dma_start(out=hext[31 * b:31 * b + 30, :], in_=st_conv[b]), w=[b_hext])
        k.dma("sp", ch_he, lambda e, b=b: e.dma_start(out=hext[31 * b + 30:31 * b + 31, :], in_=hsl[b:b + 1, :]),
              r=[b_hsl], w=[b_hext])
    b_hext.w = (ch_he.sem, ch_he.n)
    b_wrep.w = (ch_he.sem, ch_he.n)
    for b in range(4):
        k.dma("sp", out_ch, lambda e, b=b: e.dma_start(out=conv_s[b], in_=hext[31 * b + 1:31 * b + 31, :]), r=[b_hext])
    k.op("dve", lambda e: e.tensor_tensor(out=hext, in0=wrep, in1=hext, op=ALU.mult), r=[b_wrep], w=[b_hext])
    ind = talloc("q", 124, 4)
    b_ind = tbuf("ind")
    k.op("pool", lambda e: e.memset(ind, 1.0), w=[b_ind])
    k.op("pool", lambda e: e.affine_select(out=ind, in_=ind, pattern=[[-31, 4]], compare_op=ALU.is_ge, fill=0.0,
                                           base=0, channel_multiplier=1), w=[b_ind])
    k.op("pool", lambda e: e.affine_select(out=ind, in_=ind, pattern=[[31, 4]], compare_op=ALU.is_ge, fill=0.0,
                                           base=30, channel_multiplier=-1), w=[b_ind])
    cs_ = talloc("h", 4, 1024)
    b_cs = tbuf("cs_")
    for half in range(2):
        pb = nextbank(4, 6)
        k.op("pe", lambda e, pb=pb, half=half: e.matmul(PS[pb][0:4, :], lhsT=ind, rhs=hext[:, half * 512:(half + 1) * 512],
                                                        start=True, stop=True), r=[b_hext, b_ind], w=[psb[pb]])
        k.op("dve", lambda e, pb=pb, half=half: e.tensor_tensor(
            out=cs_[:, half * 512:(half + 1) * 512], in0=PS[pb][0:4, :], in1=cv4[:, 2, half * 512:(half + 1) * 512],
            op=ALU.add), r=[psb[pb], b_wrep], w=[b_cs])
    st = lnt[0:4, 16:28].rearrange("p (a s) -> p a s", a=2)
    mv = lnt[0:4, 28:30]
    rs = lnt[0:4, 30:31]
    nb_ = lnt[0:4, 31:32]
    k.op("dve", lambda e: e.bn_stats(out=st[:, 0, :], in_=cs_[:, 0:512]), r=[b_cs], w=[b_lnt])
    k.op("dve", lambda e: e.bn_stats(out=st[:, 1, :], in_=cs_[:, 512:1024]), r=[b_cs], w=[b_lnt])
    k.op("dve", lambda e: e.bn_aggr(out=mv, in_=lnt[0:4, 16:28]), w=[b_lnt])
    k.op("act", lambda e: e.activation(out=rs, in_=mv[:, 1:2], func=AF.Sqrt, bias=epsb[0:4, :]), w=[b_lnt])
    k.op("dve", lambda e: e.reciprocal(out=rs, in_=rs), w=[b_lnt])
    k.op("dve", lambda e: e.scalar_tensor_tensor(out=nb_, in0=mv[:, 0:1], scalar=-1.0, in1=rs, op0=ALU.mult,
                                                 op1=ALU.mult), w=[b_lnt])
    k.op("act", lambda e: e.activation(out=cs_, in_=cs_, func=AF.Identity, bias=nb_, scale=rs), r=[b_lnt], w=[b_cs])
    k.op("dve", lambda e: e.tensor_tensor(out=cs_, in0=cs_, in1=cv4[:, 0, :], op=ALU.mult), r=[b_wrep], w=[b_cs])
    k.op("dve", lambda e: e.tensor_tensor(out=cs_, in0=cs_, in1=cv4[:, 1, :], op=ALU.add), w=[b_cs])
    k.op("act", lambda e: e.activation(out=cs_, in_=cs_, func=AF.Silu), w=[b_cs])
    k.op("act", lambda e: e.activation(out=hsl, in_=p1[:, 2048:3072], func=AF.Silu), r=[b_p1, b_hext], w=[b_hsl])
    k.op("dve", lambda e: e.tensor_tensor(out=cs_, in0=cs_, in1=hsl, op=ALU.mult), r=[b_hsl], w=[b_cs])
    g2sT = sb("g2sT", [128, 8, 4], BF16)
    b_g2sT = Buf("g2sT")
    transpose_to_fm(cs_, b_cs, 4, g2sT[:, :, :], [b_g2sT])
    banks = outproj_tm(lambda c: g2sT[:, c, :], [b_g2sT], WoutC, b_WoutC, 4)
    ysb = talloc("q", 4, 1024)
    b_ysb = tbuf("ysb")
    for half in range(2):
        k.op("dve", lambda e, half=half, pb=banks[half]: e.scalar_tensor_tensor(
            out=ysb[:, half * 512:(half + 1) * 512], in0=x1s[:, half * 512:(half + 1) * 512], scalar=ALPHA,
            in1=PS[pb][0:4, :], op0=ALU.mult, op1=ALU.add), r=[b_x1s, psb[banks[half]]], w=[b_ysb])
    layer_norm_tm(ysb, b_ysb, 4, 2, 3)
    k.dma("sp", out_ch, lambda e: e.dma_start(out=y_s, in_=ysb), r=[b_ysb])

    finals = [(out_ch.sem, out_ch.n)] + [(c.sem, c.n) for c in ch_y + ch_kst]
    k.emit(finals)
    for cm in reversed(cms):
        cm.__exit__(None, None, None)
    return nc


def make_in_maps(inp, cores):
    f = np.float32
    g = lambda a: np.ascontiguousarray(np.asarray(a), dtype=f)
    ck = g(inp["cache_k"]).reshape(2560 * 128, 512)
    cv = g(inp["cache_v"]).reshape(2560 * 128, 512)
    sbias = g(inp["sb_bias"])[0]
    lnp = np.stack([np.broadcast_to(g(inp[n])[0], (128, D)) for n in ("ln_ab_g", "ln_ab_b", "ln_c_g", "ln_c_b")])
    cvec = np.stack([g(inp[n])[0].reshape(8, 128).T for n in ("conv_norm_g", "conv_norm_b", "b_dw")], axis=1)
    cvec4 = np.stack([np.broadcast_to(g(inp[n])[0], (4, D)) for n in ("conv_norm_g", "conv_norm_b", "b_dw")])
    wdw = g(inp["w_dw"])[0]
    wdwT = wdw.T.reshape(8, 128, 31).transpose(1, 0, 2)
    wdwrep = np.concatenate([wdw] * 4, axis=0)
    rct = np.zeros((128, 4, 16), f)
    for gi, w in enumerate((2, 4, 8, 16)):
        for t in range(16):
            rct[:, gi, t] = 1.0 / min(w, t + 1)
    pind = np.zeros((64, 4, 4), f)
    for gi, w in enumerate((2, 4, 8, 16)):
        for b in range(4):
            for r in range(16):
                v = (1.0 / w if r >= 16 - w else 0.0) - (1.0 if r == 15 else 0.0)
                pind[16 * b + r, gi, b] = v
    pind = pind.reshape(64, 16)
    common = {
        "cache_k": ck, "cache_v": cv,
        "w_in_ab": g(inp["w_in_ab"])[0], "w_out_ab": g(inp["w_out_ab"])[0],
        "w_in_c": g(inp["w_in_c"])[0], "w_out_c": g(inp["w_out_c"])[0],
        "wpool": np.ascontiguousarray(g(inp["w_pool"])[0].transpose(1, 0, 2)),
        "sbb": np.ascontiguousarray(np.broadcast_to(sbias, (128, 8))),
        "sbb512": np.ascontiguousarray(np.broadcast_to(np.tile(sbias, 64), (128, 512))),
        "lnp": np.ascontiguousarray(lnp),
        "pscale": np.ascontiguousarray(g(inp["pool_scale"])[0].T),
        "pscale4": np.ascontiguousarray(np.broadcast_to(g(inp["pool_scale"])[0].reshape(512), (4, 512))),
        "cvec": np.ascontiguousarray(cvec), "cvec4": np.ascontiguousarray(cvec4),
        "wdwT": np.ascontiguousarray(wdwT), "wdwrep": np.ascontiguousarray(wdwrep), "rct": rct, "pind": pind,
    }
    maps = []
    pt = np.asarray(inp["page_table"]).astype(np.int32)
    for c in cores:
        m = dict(common)
        m["x_p"] = g(inp["x_prompt"][c])
        m["x_s"] = g(inp["x_sample"][4 * c:4 * c + 4, 0])
        m["st_pool"] = g(inp["state_pool"][0, 4 * c:4 * c + 4])
        m["st_conv"] = g(inp["state_conv"][0, 4 * c:4 * c + 4])
        m["pt"] = np.ascontiguousarray(np.broadcast_to(pt[4 * c:4 * c + 4].reshape(256), (128, 256)))
        maps.append(m)
    return maps


def assemble(results, ncores):
    r = results
    cat = lambda name: np.stack([r[c][name] for c in range(ncores)])
    y_prompt = cat("y_p")
    y_sample = np.concatenate([r[c]["y_s"] for c in range(ncores)])[:, None, :]
    k_prompt = cat("k_p").reshape(1, ncores, S, 8, 64)
    v_prompt = cat("v_p").reshape(1, ncores, S, 8, 64)
    k_sample = np.concatenate([r[c]["k_s"] for c in range(ncores)]).reshape(1, 4 * ncores, 1, 8, 64)
    v_sample = np.concatenate([r[c]["v_s"] for c in range(ncores)]).reshape(1, 4 * ncores, 1, 8, 64)
    pool_prompt = cat("pool_p")[None]
    pool_sample = np.concatenate([r[c]["pool_s"] for c in range(ncores)])[None]
    conv_prompt = cat("conv_p")[None]
    conv_sample = np.concatenate([r[c]["conv_s"] for c in range(ncores)])[None]
    return tuple(np.ascontiguousarray(a, dtype=np.float32) for a in (
        y_prompt, y_sample, k_prompt, v_prompt, k_sample, v_sample, pool_prompt, pool_sample, conv_prompt, conv_sample))


def kernel(**inputs):
    nc = build_nc()
    cores = list(range(8))
    in_maps = make_in_maps(inputs, cores)
    res = run_bass_kernel_spmd(nc, in_maps, core_ids=cores)
    return assemble(res.results, 8)
```
